# Optimizing a Trainium2 kernel written in Bass

```python
import math
import numpy as np
import jax
import jax.numpy as jnp
from jax import lax

D_MODEL = 4096
BATCH = 8
SEQ = 2048
DEPTH = 2

HEAD_DIM = 128
N_HEADS_DIFF = 8
N_HEADS_FOX = 7
N_HEADS_NSA = 8
N_KV_NSA = 2
N_HEADS_DIL = 9
DIFF_HALF = HEAD_DIM // 2
D_FF = 11008
PLE_DIM = 256
ROPE_THETA = 10000.0
EPS = 1e-6
Q_BLOCK = 128
CMP_LEN = 32
CMP_STRIDE = 16
SEL_LEN = 64
SEL_TOPN = 16
WIN_LEN = 512
SEL_Q_BLOCK = 64
DIL_PATTERNS = ((128, 1), (512, 4), (2048, 16))
DIL_Q_BLOCK = 64
NEG_INF = -1e30
POS_BIG = 1e30

kernel_name = "hymba_style_diff_fox_nsa_dilated_macaron"


def _in_proj_sizes():
    hd = HEAD_DIM
    kv = N_KV_NSA * hd
    return (
        N_HEADS_DIFF * hd, N_HEADS_DIFF * hd, N_HEADS_DIFF * hd,
        N_HEADS_FOX * hd, N_HEADS_FOX * hd, N_HEADS_FOX * hd, N_HEADS_FOX,
        N_HEADS_NSA * hd, kv, kv, kv, kv, kv, kv, 3 * N_HEADS_NSA,
        N_HEADS_DIL * hd, N_HEADS_DIL * hd, N_HEADS_DIL * hd,
    )


def _out_width():
    return (N_HEADS_DIFF + N_HEADS_FOX + N_HEADS_NSA + N_HEADS_DIL // len(DIL_PATTERNS)) * HEAD_DIM


def rms_norm(x, g):
    xf = x.astype(jnp.float32)
    y = xf * lax.rsqrt(jnp.mean(xf * xf, axis=-1, keepdims=True) + EPS)
    return (y * g.astype(jnp.float32)).astype(x.dtype)


def rope(x, pos):
    d = x.shape[-1]
    half = d // 2
    inv = ROPE_THETA ** (-jnp.arange(half, dtype=jnp.float32) * 2.0 / d)
    ang = pos.astype(jnp.float32)[:, None] * inv[None, :]
    cos = jnp.cos(ang)[None, :, None, :]
    sin = jnp.sin(ang)[None, :, None, :]
    xf = x.astype(jnp.float32)
    x1, x2 = xf[..., :half], xf[..., half:]
    return jnp.concatenate([x1 * cos - x2 * sin, x2 * cos + x1 * sin], axis=-1).astype(x.dtype)


def swiglu(x, w_gate, w_up, w_down):
    return (jax.nn.silu(x @ w_gate) * (x @ w_up)) @ w_down


def masked_softmax(s, valid):
    s = jnp.where(valid, s, NEG_INF)
    m = jnp.max(s, axis=-1, keepdims=True)
    e = jnp.where(valid, jnp.exp(s - m), 0.0)
    den = jnp.sum(e, axis=-1, keepdims=True)
    return e / jnp.where(den > 0, den, 1.0), m, den


def sweep_blocks(fn, n_blocks):
    out = lax.map(fn, jnp.arange(n_blocks))
    out = jnp.moveaxis(out, 0, 1)
    return out.reshape((out.shape[0], out.shape[1] * out.shape[2]) + out.shape[3:])


def diff_attention(q, k, v, lam, lam_init, subln_g):
    B, T, H, _, dq = q.shape
    scale = dq ** -0.5
    vf = v.astype(jnp.float32)
    kpos = jnp.arange(T)

    def block(i):
        t0 = i * Q_BLOCK
        qb = lax.dynamic_slice_in_dim(q, t0, Q_BLOCK, axis=1)
        s = jnp.einsum('bqhcd,bkhcd->bhcqk', qb, k, preferred_element_type=jnp.float32) * scale
        valid = kpos[None, :] <= (t0 + jnp.arange(Q_BLOCK))[:, None]
        pr, _, _ = masked_softmax(s, valid)
        a = pr[:, :, 0] - lam * pr[:, :, 1]
        o = jnp.einsum('bhqk,bkhd->bqhd', a, vf)
        return rms_norm(o, subln_g) * (1.0 - lam_init)

    return sweep_blocks(block, T // Q_BLOCK).astype(v.dtype)


def forgetting_attention(q, k, v, f_logit):
    B, T, H, dh = q.shape
    scale = dh ** -0.5
    c = jnp.cumsum(jax.nn.log_sigmoid(f_logit.astype(jnp.float32)), axis=1)
    c_k = jnp.transpose(c, (0, 2, 1))[:, :, None, :]
    vf = v.astype(jnp.float32)
    kpos = jnp.arange(T)

    def block(i):
        t0 = i * Q_BLOCK
        qb = lax.dynamic_slice_in_dim(q, t0, Q_BLOCK, axis=1)
        c_q = jnp.transpose(lax.dynamic_slice_in_dim(c, t0, Q_BLOCK, axis=1), (0, 2, 1))[..., None]
        s = jnp.einsum('bqhd,bkhd->bhqk', qb, k, preferred_element_type=jnp.float32) * scale + (c_q - c_k)
        valid = kpos[None, :] <= (t0 + jnp.arange(Q_BLOCK))[:, None]
        pr, _, _ = masked_softmax(s, valid)
        return jnp.einsum('bhqk,bkhd->bqhd', pr, vf)

    return sweep_blocks(block, T // Q_BLOCK).astype(v.dtype)


def nsa_attention(q, k_cmp_raw, v_cmp_raw, k_sel, v_sel, k_win, v_win, gates,
                  pe_k, pe_v, wk1, wk2, wv1, wv2):
    B, T, H, dh = q.shape
    G = k_sel.shape[2]
    R = H // G
    scale = dh ** -0.5
    qg = q.reshape(B, T, G, R, dh)
    tpos = jnp.arange(T)

    n_cmp = (T - CMP_LEN) // CMP_STRIDE + 1
    cmp_idx = np.arange(n_cmp)[:, None] * CMP_STRIDE + np.arange(CMP_LEN)[None, :]

    def compress(xk, pe, w1, w2):
        blocks = xk[:, cmp_idx] + pe[None, None, :, None, :]
        flat = jnp.moveaxis(blocks, 3, 2).reshape(B, n_cmp, G, CMP_LEN * dh)
        return jax.nn.gelu(flat @ w1) @ w2

    kc = compress(k_cmp_raw, pe_k, wk1, wk2)
    vc = compress(v_cmp_raw, pe_v, wv1, wv2)
    cmp_valid = jnp.asarray(cmp_idx[:, -1])[None, :] <= tpos[:, None]
    s_cmp = jnp.einsum('btgrd,bcgd->bgrtc', qg, kc, preferred_element_type=jnp.float32) * scale
    p_cmp, _, _ = masked_softmax(s_cmp, cmp_valid)
    o_cmp = jnp.einsum('bgrtc,bcgd->btgrd', p_cmp, vc.astype(jnp.float32))

    n_sel = T // SEL_LEN
    sel_start = np.arange(n_sel) * SEL_LEN
    cmp_start = np.arange(n_cmp) * CMP_STRIDE
    overlap = ((cmp_start[None, :] < sel_start[:, None] + SEL_LEN) &
               (cmp_start[None, :] + CMP_LEN > sel_start[:, None])).astype(np.float32)
    imp = jnp.einsum('bgrtc,sc->bgts', p_cmp, jnp.asarray(overlap))
    blk = jnp.arange(n_sel)[None, :]
    cur = (tpos // SEL_LEN)[:, None]
    forced = (blk == 0) | (blk == cur) | (blk == cur - 1)
    causal_blk = jnp.asarray(sel_start)[None, :] <= tpos[:, None]
    imp = jnp.where(forced, POS_BIG, jnp.where(causal_blk, imp, NEG_INF))
    n_top = min(SEL_TOPN, n_sel)
    _, sel_idx = lax.top_k(imp, n_top)

    ks_blk = jnp.transpose(k_sel.reshape(B, n_sel, SEL_LEN, G, dh), (0, 3, 1, 2, 4))
    vs_blk = jnp.transpose(v_sel.reshape(B, n_sel, SEL_LEN, G, dh), (0, 3, 1, 2, 4))
    b_ix = jnp.arange(B)[:, None, None, None]
    g_ix = jnp.arange(G)[None, :, None, None]

    def sel_block(i):
        t0 = i * SEL_Q_BLOCK
        qb = lax.dynamic_slice_in_dim(qg, t0, SEL_Q_BLOCK, axis=1)
        ib = lax.dynamic_slice_in_dim(sel_idx, t0, SEL_Q_BLOCK, axis=2)
        kb = ks_blk[b_ix, g_ix, ib].reshape(B, G, SEL_Q_BLOCK, n_top * SEL_LEN, dh)
        vb = vs_blk[b_ix, g_ix, ib].reshape(B, G, SEL_Q_BLOCK, n_top * SEL_LEN, dh)
        kpos = (ib[..., None] * SEL_LEN + jnp.arange(SEL_LEN)).reshape(B, G, SEL_Q_BLOCK, n_top * SEL_LEN)
        valid = kpos <= (t0 + jnp.arange(SEL_Q_BLOCK))[None, None, :, None]
        s = jnp.einsum('bqgrd,bgqkd->bgrqk', qb, kb, preferred_element_type=jnp.float32) * scale
        pr, _, _ = masked_softmax(s, valid[:, :, None])
        return jnp.einsum('bgrqk,bgqkd->bqgrd', pr, vb.astype(jnp.float32))

    o_sel = sweep_blocks(sel_block, T // SEL_Q_BLOCK)

    kw_pad = jnp.pad(k_win, ((0, 0), (WIN_LEN, 0), (0, 0), (0, 0)))
    vw_pad = jnp.pad(v_win, ((0, 0), (WIN_LEN, 0), (0, 0), (0, 0)))

    def win_block(i):
        t0 = i * Q_BLOCK
        qb = lax.dynamic_slice_in_dim(qg, t0, Q_BLOCK, axis=1)
        kb = lax.dynamic_slice_in_dim(kw_pad, t0, WIN_LEN + Q_BLOCK, axis=1)
        vb = lax.dynamic_slice_in_dim(vw_pad, t0, WIN_LEN + Q_BLOCK, axis=1)
        kpos = t0 - WIN_LEN + jnp.arange(WIN_LEN + Q_BLOCK)
        tq = (t0 + jnp.arange(Q_BLOCK))[:, None]
        valid = (kpos[None, :] <= tq) & (kpos[None, :] > tq - WIN_LEN) & (kpos[None, :] >= 0)
        s = jnp.einsum('bqgrd,bkgd->bgrqk', qb, kb, preferred_element_type=jnp.float32) * scale
        pr, _, _ = masked_softmax(s, valid)
        return jnp.einsum('bgrqk,bkgd->bqgrd', pr, vb.astype(jnp.float32))

    o_win = sweep_blocks(win_block, T // Q_BLOCK)

    g = gates.astype(jnp.float32).reshape(B, T, G, R, 3)
    o = g[..., 0:1] * o_cmp + g[..., 1:2] * o_sel + g[..., 2:3] * o_win
    return o.reshape(B, T, H, dh).astype(q.dtype)


def dilated_attention(q, k, v):
    B, T, H, dh = q.shape
    n_g = len(DIL_PATTERNS)
    hg = H // n_g
    scale = dh ** -0.5
    kg = k.reshape(B, T, n_g, hg, dh)
    vg = v.astype(jnp.float32).reshape(B, T, n_g, hg, dh)

    def block(i):
        t0 = i * DIL_Q_BLOCK
        qb = lax.dynamic_slice_in_dim(q, t0, DIL_Q_BLOCK, axis=1).reshape(B, DIL_Q_BLOCK, n_g, hg, dh)
        tq = t0 + jnp.arange(DIL_Q_BLOCK)
        outs, lses = [], []
        for g, (w, r) in enumerate(DIL_PATTERNS):
            nk = w // r + 1
            kpos = tq[:, None] - jnp.arange(nk)[None, :] * r
            valid = kpos >= 0
            kidx = jnp.maximum(kpos, 0)
            kb = kg[:, :, g][:, kidx]
            vb = vg[:, :, g][:, kidx]
            s = jnp.einsum('bqhd,bqkhd->bhqk', qb[:, :, g], kb, preferred_element_type=jnp.float32) * scale
            pr, m, den = masked_softmax(s, valid)
            outs.append(jnp.einsum('bhqk,bqkhd->bqhd', pr, vb))
            lses.append(jnp.transpose((m + jnp.log(den))[..., 0], (0, 2, 1)))
        alpha = jax.nn.softmax(jnp.stack(lses, axis=0), axis=0)[..., None]
        return jnp.sum(alpha * jnp.stack(outs, axis=0), axis=0)

    return sweep_blocks(block, T // DIL_Q_BLOCK).astype(q.dtype)


def setup_inputs(seed: int = 0) -> dict:
    key = jax.random.key(seed)
    keys = iter(jax.random.split(key, 48))

    def nrm(shape, scale):
        return scale * jax.random.normal(next(keys), shape, dtype=jnp.float32)

    def gain(shape):
        return 1.0 + 0.05 * jax.random.normal(next(keys), shape, dtype=jnp.float32)

    d, f, hd, L = D_MODEL, D_FF, HEAD_DIM, DEPTH
    n_in = sum(_in_proj_sizes())
    n_out = _out_width()
    return {
        "x": nrm((BATCH, SEQ, d), 1.0),
        "p": nrm((DEPTH, BATCH, SEQ, PLE_DIM), 1.0),
        "ffn1_pre_g": gain((L, d)),
        "ffn1_w_gate": nrm((L, d, f), d ** -0.5),
        "ffn1_w_up": nrm((L, d, f), d ** -0.5),
        "ffn1_w_down": nrm((L, f, d), f ** -0.5),
        "ffn1_post_g": gain((L, d)),
        "mix_pre_g": gain((L, d)),
        "w_in": nrm((L, d, n_in), d ** -0.5),
        "fox_bf": jnp.linspace(1.0, 6.0, N_HEADS_FOX, dtype=jnp.float32)[None, :] + nrm((L, N_HEADS_FOX), 0.1),
        "diff_lam_q1": nrm((L, DIFF_HALF), 0.1),
        "diff_lam_k1": nrm((L, DIFF_HALF), 0.1),
        "diff_lam_q2": nrm((L, DIFF_HALF), 0.1),
        "diff_lam_k2": nrm((L, DIFF_HALF), 0.1),
        "diff_subln_g": gain((L, hd)),
        "nsa_pe_k": nrm((L, CMP_LEN, hd), 0.1),
        "nsa_pe_v": nrm((L, CMP_LEN, hd), 0.1),
        "nsa_wk1": nrm((L, CMP_LEN * hd, hd), (CMP_LEN * hd) ** -0.5),
        "nsa_wk2": nrm((L, hd, hd), hd ** -0.5),
        "nsa_wv1": nrm((L, CMP_LEN * hd, hd), (CMP_LEN * hd) ** -0.5),
        "nsa_wv2": nrm((L, hd, hd), hd ** -0.5),
        "w_out": nrm((L, n_out, d), n_out ** -0.5),
        "mix_post_g": gain((L, d)),
        "ffn2_pre_g": gain((L, d)),
        "ffn2_w_gate": nrm((L, d, f), d ** -0.5),
        "ffn2_w_up": nrm((L, d, f), d ** -0.5),
        "ffn2_w_down": nrm((L, f, d), f ** -0.5),
        "ffn2_post_g": gain((L, d)),
        "ple_pre_g": gain((L, d)),
        "ple_w_gate": nrm((L, d, d), d ** -0.5),
        "ple_w_proj": nrm((L, PLE_DIM, d), PLE_DIM ** -0.5),
        "ple_post_g": gain((L, d)),
    }


def reference(x, p, ffn1_pre_g, ffn1_w_gate, ffn1_w_up, ffn1_w_down, ffn1_post_g,
              mix_pre_g, w_in, fox_bf, diff_lam_q1, diff_lam_k1, diff_lam_q2, diff_lam_k2,
              diff_subln_g, nsa_pe_k, nsa_pe_v, nsa_wk1, nsa_wk2, nsa_wv1, nsa_wv2,
              w_out, mix_post_g, ffn2_pre_g, ffn2_w_gate, ffn2_w_up, ffn2_w_down, ffn2_post_g,
              ple_pre_g, ple_w_gate, ple_w_proj, ple_post_g):
    B, T, _ = x.shape
    hd = HEAD_DIM
    pos = jnp.arange(T)
    offsets = np.cumsum(_in_proj_sizes())[:-1].tolist()
    h = x
    for li in range(DEPTH):
        f1 = swiglu(rms_norm(h, ffn1_pre_g[li]), ffn1_w_gate[li], ffn1_w_up[li], ffn1_w_down[li])
        h = h + 0.5 * rms_norm(f1, ffn1_post_g[li])

        u = rms_norm(h, mix_pre_g[li])
        z = u @ w_in[li]
        (qa, ka, va, qb, kb, vb, fb, qc, kcc, vcc, ksc, vsc, kwc, vwc, gc,
         qd, kd, vd) = jnp.split(z, offsets, axis=-1)

        lam_init = 0.8 - 0.6 * math.exp(-0.3 * li)
        lam = (jnp.exp(jnp.sum(diff_lam_q1[li].astype(jnp.float32) * diff_lam_k1[li].astype(jnp.float32)))
               - jnp.exp(jnp.sum(diff_lam_q2[li].astype(jnp.float32) * diff_lam_k2[li].astype(jnp.float32)))
               + lam_init)
        qa = rope(qa.reshape(B, T, 2 * N_HEADS_DIFF, DIFF_HALF), pos).reshape(B, T, N_HEADS_DIFF, 2, DIFF_HALF)
        ka = rope(ka.reshape(B, T, 2 * N_HEADS_DIFF, DIFF_HALF), pos).reshape(B, T, N_HEADS_DIFF, 2, DIFF_HALF)
        o_a = diff_attention(qa, ka, va.reshape(B, T, N_HEADS_DIFF, hd), lam, lam_init, diff_subln_g[li])

        o_b = forgetting_attention(qb.reshape(B, T, N_HEADS_FOX, hd), kb.reshape(B, T, N_HEADS_FOX, hd),
                                   vb.reshape(B, T, N_HEADS_FOX, hd), fb + fox_bf[li])

        kvs = (B, T, N_KV_NSA, hd)
        o_c = nsa_attention(rope(qc.reshape(B, T, N_HEADS_NSA, hd), pos),
                            rope(kcc.reshape(kvs), pos), vcc.reshape(kvs),
                            rope(ksc.reshape(kvs), pos), vsc.reshape(kvs),
                            rope(kwc.reshape(kvs), pos), vwc.reshape(kvs),
                            jax.nn.sigmoid(gc.reshape(B, T, N_HEADS_NSA, 3)),
                            nsa_pe_k[li], nsa_pe_v[li], nsa_wk1[li], nsa_wk2[li], nsa_wv1[li], nsa_wv2[li])

        o_d = dilated_attention(rope(qd.reshape(B, T, N_HEADS_DIL, hd), pos),
                                rope(kd.reshape(B, T, N_HEADS_DIL, hd), pos),
                                vd.reshape(B, T, N_HEADS_DIL, hd))

        o = jnp.concatenate([o_a.reshape(B, T, -1), o_b.reshape(B, T, -1),
                             o_c.reshape(B, T, -1), o_d.reshape(B, T, -1)], axis=-1)
        h = h + rms_norm(o @ w_out[li], mix_post_g[li])

        f2 = swiglu(rms_norm(h, ffn2_pre_g[li]), ffn2_w_gate[li], ffn2_w_up[li], ffn2_w_down[li])
        h = h + 0.5 * rms_norm(f2, ffn2_post_g[li])

        gate = jax.nn.sigmoid(rms_norm(h, ple_pre_g[li]) @ ple_w_gate[li])
        h = h + rms_norm(gate * (p[li] @ ple_w_proj[li]), ple_post_g[li])
    return h
```

```python
import numpy as np
import concourse.bass as bass
import concourse.mybir as mybir
from concourse.ap import AP
from concourse.bass_utils import run_bass_kernel_spmd

F32 = mybir.dt.float32
BF16 = mybir.dt.bfloat16
AF = mybir.ActivationFunctionType
ALU = mybir.AluOpType
AX = mybir.AxisListType

T = 2048
D = 4096
DFF = 11008
NL = 2
KC = D // 128
FC = DFF // 128
TT = 512
NTT = T // TT
EPS = 1e-6
NIN = 11807
NOUT = 3328


class Buf:
    __slots__ = ("name", "lw", "rd")

    def __init__(self, name):
        self.name = name
        self.lw = None
        self.rd = {}


class _Rec:
    def __init__(self):
        self.call = None

    def __getattr__(self, name):
        def f(*a, **k):
            self.call = (name, a, k)
            return self
        return f


ENGS = ("pe", "act", "dve", "pool", "sp")
SEM_CH = 30000


class Prog:
    def __init__(self, nc, nslots=None):
        self.nc = nc
        self.ops = {e: [] for e in ENGS}
        self.seen = {e: {} for e in ENGS}
        self.nslots = nslots or {"sp": 24, "act": 8, "pool": 12}
        self.dma_cnt = {(q, s): 0 for q in self.nslots for s in range(self.nslots[q])}
        self.dma_rr = {q: 0 for q in self.nslots}
        self.ncomp = {e: 0 for e in ENGS}

    def _collect(self, E, reads, writes):
        deps = []
        for b in reads:
            if b.lw is not None:
                deps.append(b.lw)
        for b in writes:
            if b.lw is not None:
                deps.append(b.lw)
            deps.extend(b.rd.values())
        return deps

    def _filter(self, E, deps):
        waits = []
        seen = self.seen[E]
        for ev in deps:
            if ev[0] == "c":
                _, F, i = ev
                if F == E and E == "pe":
                    continue
                if seen.get(F, -1) >= i:
                    continue
                seen[F] = i
                waits.append(ev)
            else:
                _, q, s, cnt = ev
                key = (q, s)
                if seen.get(key, 0) >= cnt:
                    continue
                seen[key] = cnt
                waits.append(ev)
        return waits

    def _commit(self, ev, reads, writes):
        for b in reads:
            k = ev[1] if ev[0] == "c" else (ev[1], ev[2])
            b.rd[k] = ev
        for b in writes:
            b.lw = ev
            b.rd = {}

    def op(self, E, fn, reads=(), writes=()):
        rec = _Rec()
        fn(rec)
        name, a, k = rec.call
        fn = (lambda e, name=name, a=a, k=k: getattr(e, name)(*a, **k))
        waits = self._filter(E, self._collect(E, reads, writes))
        i = self.ncomp[E]
        self.ncomp[E] += 1
        ev = ("c", E, i)
        self.ops[E].append({"fn": fn, "waits": waits, "ev": ev})
        self._commit(ev, reads, writes)

    def dma(self, Q, out, in_, reads=(), writes=(), **kw):
        s = self.dma_rr[Q]
        self.dma_rr[Q] = (s + 1) % self.nslots[Q]
        cnt = self.dma_cnt[(Q, s)] + 16
        self.dma_cnt[(Q, s)] = cnt
        deps = self._collect(Q, reads, writes)
        if cnt > 16:
            deps.append(("d", Q, s, cnt - 16))
        waits = self._filter(Q, deps)
        ev = ("d", Q, s, cnt)
        self.ops[Q].append({"fn": (lambda e, o=out, i=in_, k=kw: e.dma_start(out=o, in_=i, **k)),
                            "waits": waits, "ev": ev})
        self._commit(ev, reads, writes)

    def dma_custom(self, Q, fn, reads=(), writes=()):
        s = self.dma_rr[Q]
        self.dma_rr[Q] = (s + 1) % self.nslots[Q]
        cnt = self.dma_cnt[(Q, s)] + 16
        self.dma_cnt[(Q, s)] = cnt
        deps = self._collect(Q, reads, writes)
        if cnt > 16:
            deps.append(("d", Q, s, cnt - 16))
        waits = self._filter(Q, deps)
        ev = ("d", Q, s, cnt)
        self.ops[Q].append({"fn": fn, "waits": waits, "ev": ev})
        self._commit(ev, reads, writes)

    def barrier(self):
        evs = []
        for F in ENGS:
            if self.ncomp[F] > 0:
                evs.append(("c", F, self.ncomp[F] - 1))
        for (q, s), cnt in self.dma_cnt.items():
            if cnt > 0:
                evs.append(("d", q, s, cnt))
        for E in ENGS:
            waits = self._filter(E, [ev for ev in evs if not (ev[0] == "c" and ev[1] == E and E == "pe")])
            if waits:
                self.ops[E].append({"fn": None, "waits": waits, "ev": None})

    def emit(self):
        nc = self.nc
        ms = {e: set() for e in ENGS}
        for E in ENGS:
            for o in self.ops[E]:
                for w in o["waits"]:
                    if w[0] == "c":
                        ms[w[1]].add(w[2])
        rank = {}
        nsem_c = {}
        for E in ENGS:
            srt = sorted(ms[E])
            rank[E] = {i: k for k, i in enumerate(srt)}
            nsem_c[E] = (len(srt) + SEM_CH - 1) // SEM_CH
        self.stats = {E: (len(self.ops[E]), len(ms[E])) for E in ENGS}
        import contextlib
        with contextlib.ExitStack() as st:
            csem = {E: [st.enter_context(nc.semaphore(f"c_{E}_{j}")) for j in range(nsem_c[E])] for E in ENGS}
            dsem = {k: st.enter_context(nc.semaphore(f"d_{k[0]}_{k[1]}")) for k in self.dma_cnt if self.dma_cnt[k] > 0}
            block = st.enter_context(nc.Block())

            def run(E, eng):
                for o in self.ops[E]:
                    for w in o["waits"]:
                        if w[0] == "c":
                            k = rank[w[1]][w[2]]
                            eng.wait_ge(csem[w[1]][k // SEM_CH], (k % SEM_CH) + 1)
                        else:
                            eng.wait_ge(dsem[(w[1], w[2])], w[3])
                    if o["fn"] is None:
                        continue
                    ins = o["fn"](eng)
                    ev = o["ev"]
                    if ev[0] == "c":
                        if ev[2] in rank[E]:
                            k = rank[E][ev[2]]
                            ins.then_inc(csem[E][k // SEM_CH], 1)
                    else:
                        ins.then_inc(dsem[(ev[1], ev[2])], 16)

            @block.tensor
            def _(eng):
                run("pe", eng)

            @block.scalar
            def _(eng):
                run("act", eng)

            @block.vector
            def _(eng):
                run("dve", eng)

            @block.gpsimd
            def _(eng):
                run("pool", eng)

            @block.sync
            def _(eng):
                run("sp", eng)


GAIN_NAMES = ["ffn1_pre_g", "ffn1_post_g", "mix_pre_g", "mix_post_g", "ffn2_pre_g", "ffn2_post_g",
              "ple_pre_g", "ple_post_g"]

C_IDENT = 0
C_ONES = 128
C_RT64 = 256
C_RT32 = 384
C_TRIU = 512
C_SEL64 = 640
C_OVL = 768
C_ATAB = 800
NCONST = 800 + 512
M_CAUS = 0
M_WIN = 2048
M_U = 4096
M_CMP = 4224
M_EXP = 6272
NMASK = 8320

ZQK = dict(qa=0, ka=8, qb=16, kb=23, qc=30, kcc=38, vcc=40, ksc=42, kwc=44, qd=46, kd=55)
NQK = 64
FM_JOBS = [("qa", 0, 8, 32), ("ka", 1024, 8, 32), ("qb", 3072, 7, 0), ("kb", 3968, 7, 0), ("qc", 5767, 8, 64),
           ("kcc", 6791, 2, 64), ("vcc", 7047, 2, 0), ("ksc", 7303, 2, 64), ("kwc", 7815, 2, 64),
           ("qd", 8351, 9, 64), ("kd", 9503, 9, 64)]
GC_COL = 8327
TM_JOBS = [(2048, 512, 0, 0), (2560, 512, 512, 0), (4864, 512, 1024, 0), (5376, 391, 1536, 7), (7559, 256, 1920, 0),
           (8071, 256, 2176, 0), (10655, 512, 2432, 0), (11167, 512, 2944, 0), (11679, 128, 3456, 0)]
ZV = dict(va=0, vb=1024, vsc=1920, vwc=2176, vd=2432)
NV = 3584


def make_consts():
    c = np.zeros((128, NCONST), np.float32)
    c[:, C_IDENT:C_IDENT + 128] = np.eye(128, dtype=np.float32)
    c[:, C_ONES:C_ONES + 128] = 1.0
    for base, half in ((C_RT64, 64), (C_RT32, 32)):
        R = np.zeros((128, 128), np.float32)
        for m in range(128):
            l = m % (2 * half)
            if l < half:
                R[m, m + half] = -1.0
            else:
                R[m, m - half] = 1.0
        c[:, base:base + 128] = R.T
    k = np.arange(128)
    c[:, C_TRIU:C_TRIU + 128] = (k[:, None] <= k[None, :]).astype(np.float32)
    c[64, C_SEL64:C_SEL64 + 128] = 1.0
    n_cmp, n_sel = 127, 32
    cs = np.arange(n_cmp) * 16
    ss = np.arange(n_sel) * 64
    ovl = ((cs[None, :] < ss[:, None] + 64) & (cs[None, :] + 32 > ss[:, None])).astype(np.float32)
    c[:n_cmp, C_OVL:C_OVL + 32] = ovl.T
    t = np.arange(T)
    blk = np.arange(32)[None, :]
    cur = (t // 64)[:, None]
    forced = (blk == 0) | (blk == cur) | (blk == cur - 1)
    causal = (ss[None, :] <= t[:, None])
    A = np.where(forced, 1e30, np.where(causal, 0.0, -1e30)).astype(np.float32)
    c[:, C_ATAB:C_ATAB + 512] = A.reshape(16, 128, 32).transpose(1, 0, 2).reshape(128, 512)
    return c


def make_masks():
    import ml_dtypes
    m = np.zeros((128, NMASK), np.float32)
    k = np.arange(128)[:, None]
    q = np.arange(512)[None, :]
    for j in range(4):
        Mj = (128 * j + k <= q).astype(np.float32)
        m[:, M_CAUS + j * 512:M_CAUS + (j + 1) * 512] = Mj
        m[:, M_WIN + j * 512:M_WIN + (j + 1) * 512] = 1.0 - Mj
    m[:, M_U:M_U + 128] = (np.arange(128)[None, :] <= k).astype(np.float32)
    cc = np.arange(127)[:, None]
    tt = np.arange(T)[None, :]
    m[:127, M_CMP:M_CMP + T] = (16 * cc + 31 <= tt).astype(np.float32)
    s_ = np.arange(32)[:, None]
    m[:32, M_EXP:M_EXP + T] = (tt // 64 == s_).astype(np.float32)
    return m.astype(ml_dtypes.bfloat16)


def make_rope():
    r = np.zeros((4, 128, T), np.float32)
    pos = np.arange(T, dtype=np.float32)
    for i, (half, d) in enumerate(((64, 128), (32, 64))):
        inv = (np.float32(10000.0) ** (-np.arange(half, dtype=np.float32) * np.float32(2.0) / np.float32(d))).astype(np.float32)
        ang = (pos[:, None] * inv[None, :]).astype(np.float32)
        p = np.arange(128) % half
        r[2 * i] = np.cos(ang)[:, p].T
        r[2 * i + 1] = np.sin(ang)[:, p].T
    return r


class Arena:
    def __init__(self, nc, nbytes):
        self.t = nc.alloc_sbuf_tensor("arena", [128, nbytes // 2], BF16)
        self.nbytes = nbytes
        self.off = 0

    def alloc(self, shape, dtype):
        esz = 4 if dtype == F32 else 2
        n = int(np.prod(shape[1:]))
        nb = (n * esz + 31) // 32 * 32
        assert self.off + nb <= self.nbytes, (self.off, nb, self.nbytes)
        a = self.t[0:shape[0], self.off // 2:(self.off + n * esz) // 2]
        if dtype == F32:
            a = a.bitcast(F32)
        self.off += nb
        if len(shape) == 3:
            a = a.rearrange("p (a b) -> p a b", b=shape[2])
        elif len(shape) == 4:
            a = a.rearrange("p (a b c) -> p a b c", b=shape[2], c=shape[3])
        return a


def bcast_free(ap2d, n, pos=1):
    l = [list(x) for x in ap2d.ap]
    if pos == 1:
        l = [l[0], [0, n]] + l[1:]
    else:
        l = l + [[0, n]]
    return AP(ap2d.tensor, ap2d.offset, l)


def build(mode="full", n_layers=NL, phases=("ffn1", "mix", "ffn2", "ple"), stage=9, ntiles=NTT, heads=None):
    nc = bass.Bass("TRN2", target_bir_lowering=False)
    P = Prog(nc)

    def din(name, shape, dtype=F32):
        return nc.dram_tensor(name, list(shape), dtype, kind="ExternalInput")

    def dscr(name, shape, dtype, io=None):
        kind = {"in": "ExternalInput", "out": "ExternalOutput", None: "Internal"}[io]
        return nc.dram_tensor(name, list(shape), dtype, kind=kind)

    x_d = din("x", [T, D])
    gains_d = din("gains", [128, NL * 8 * KC])
    consts_d = din("consts", [128, NCONST])
    w = {}
    full = mode == "full"
    if full or mode == "ffn":
        for nm, shp in [("ffn1_w_gate", [NL, D, DFF]), ("ffn1_w_up", [NL, D, DFF]), ("ffn1_w_down", [NL, DFF, D]),
                        ("ffn2_w_gate", [NL, D, DFF]), ("ffn2_w_up", [NL, D, DFF]), ("ffn2_w_down", [NL, DFF, D])]:
            w[nm] = din(nm, shp)
    if full or mode in ("win", "attn", "wout", "mix"):
        masks_d = din("masks", [128, NMASK], BF16)
        rope_d = din("rope", [4, 128, T])
        for nm, shp in ([("w_in", [NL, D, NIN])] if mode != "attn" else []) + [("fox_bf", [NL, 7]), ("diff_lam_q1", [NL, 64]), ("diff_lam_k1", [NL, 64]),
                        ("diff_lam_q2", [NL, 64]), ("diff_lam_k2", [NL, 64]), ("diff_subln_g", [NL, 128]),
                        ("nsa_pe_k", [NL, 32, 128]), ("nsa_pe_v", [NL, 32, 128]), ("nsa_wk1", [NL, 4096, 128]),
                        ("nsa_wk2", [NL, 128, 128]), ("nsa_wv1", [NL, 4096, 128]), ("nsa_wv2", [NL, 128, 128])] + \
                ([("w_out", [NL, NOUT, D])] if mode not in ("attn", "win") else []):
            w[nm] = din(nm, shp)
    if full or mode == "ple":
        p_d = din("p", [NL, T, 256])
        w["ple_w_gate"] = din("ple_w_gate", [NL, D, D])
        w["ple_w_proj"] = din("ple_w_proj", [NL, 256, D])
    out_d = nc.dram_tensor("out", [T, D], F32, kind="ExternalOutput")
    hT_d = nc.dram_tensor("hT", [KC, 128, T], F32, kind="Internal")
    yT_d = nc.dram_tensor("yT", [KC, 128, T], F32, kind="Internal")
    hT_b = [[Buf(f"hT{tt}_{kc}") for kc in range(KC)] for tt in range(NTT)]
    yT_b = [Buf(f"yT{kc}") for kc in range(KC)]
    sio_w = "out" if mode == "win" else ("in" if mode == "attn" else None)
    zqk_d = dscr("zqk", [NQK, 128, T], BF16, sio_w)
    zv_d = dscr("zv", [T, NV], BF16, sio_w)
    zf_d = dscr("zf", [T, 8], F32, sio_w)
    gT_d = dscr("gT", [24, T], F32, sio_w)
    oT_d = dscr("oT", [26, 128, T], BF16, "out" if mode == "attn" else None)
    zqk_b = [Buf(f"zqk{i}") for i in range(NQK)]
    zv_b = Buf("zv")
    zf_b = Buf("zf")
    gT_b = Buf("gT")
    oT_b = [Buf(f"oT{i}") for i in range(26)]

    ar = Arena(nc, 204 * 1024)
    psum = [nc.alloc_psum_tensor(f"ps{i}", [128, 512], F32).ap() for i in range(8)]

    consts = ar.alloc([128, NCONST], F32)
    gains = ar.alloc([128, NL * 8 * KC], F32)
    ident = consts[:, C_IDENT:C_IDENT + 128]
    ones_f = consts[:, C_ONES:C_ONES + 128]
    ones_bf = ar.alloc([128, 128], BF16)
    eps_t = ar.alloc([128, 1], F32)
    one_t = ar.alloc([128, 1], F32)
    PERSIST = ar.off
    cb = Buf("consts")
    P.dma("sp", consts, consts_d.ap(), writes=[cb])
    P.dma("sp", gains, gains_d.ap(), writes=[cb])
    P.op("dve", lambda e: e.memset(ones_bf, 1.0), writes=[cb])
    P.op("dve", lambda e: e.memset(eps_t, EPS), writes=[cb])
    P.op("dve", lambda e: e.memset(one_t, 1.0), writes=[cb])

    def gcol(li, gi, kc):
        return gains[:, (li * 8 + gi) * KC + kc:(li * 8 + gi) * KC + kc + 1]

    def phase_reset():
        P.barrier()
        ar.off = PERSIST
        return [Buf(f"ps{i}") for i in range(8)]

    def load_x():
        psb = phase_reset()
        xs = [ar.alloc([128, D], F32) for _ in range(2)]
        xsb = [Buf("xs0"), Buf("xs1")]
        hs = [ar.alloc([128, KC, 128], F32) for _ in range(2)]
        hsb = [Buf("hs0"), Buf("hs1")]
        for ts in range(T // 128):
            xb, xbb = xs[ts % 2], xsb[ts % 2]
            hb, hbb = hs[ts % 2], hsb[ts % 2]
            P.dma("sp", xb, x_d[ts * 128:(ts + 1) * 128, :], writes=[xbb])
            for g4 in range(KC // 4):
                pi = g4 % 8
                for j in range(4):
                    kc = g4 * 4 + j
                    P.op("pe", lambda e, o=psum[pi][:, j * 128:(j + 1) * 128], i=xb[:, kc * 128:(kc + 1) * 128]:
                         e.transpose(o, i, ident), reads=[xbb, cb], writes=[psb[pi]])
                if g4 % 2 == 0:
                    P.op("act", lambda e, o=hb[:, g4 * 4:(g4 + 1) * 4, :], i=psum[pi]:
                         e.copy(o, i.rearrange("p (a b) -> p a b", b=128)), reads=[psb[pi]], writes=[hbb])
                else:
                    P.op("dve", lambda e, o=hb[:, g4 * 4:(g4 + 1) * 4, :], i=psum[pi]:
                         e.tensor_copy(o, i.rearrange("p (a b) -> p a b", b=128)), reads=[psb[pi]], writes=[hbb])
            P.dma("pool", hT_d.ap().rearrange("k p t -> p k t")[:, :, ts * 128:(ts + 1) * 128], hb,
                  reads=[hbb], writes=hT_b[ts // 4])

    def store_out():
        psb = phase_reset()
        hs = [ar.alloc([128, KC, 128], F32) for _ in range(2)]
        hsb = [Buf("hs0"), Buf("hs1")]
        xs = [ar.alloc([128, D], F32) for _ in range(2)]
        xsb = [Buf("xs0"), Buf("xs1")]
        for ts in range(T // 128):
            xb, xbb = xs[ts % 2], xsb[ts % 2]
            hb, hbb = hs[ts % 2], hsb[ts % 2]
            P.dma("sp", hb, hT_d.ap().rearrange("k p t -> p k t")[:, :, ts * 128:(ts + 1) * 128],
                  reads=hT_b[ts // 4], writes=[hbb])
            for g4 in range(KC // 4):
                pi = g4 % 8
                for j in range(4):
                    kc = g4 * 4 + j
                    P.op("pe", lambda e, o=psum[pi][:, j * 128:(j + 1) * 128], i=hb[:, kc, :]:
                         e.transpose(o, i, ident), reads=[hbb, cb], writes=[psb[pi]])
                if g4 % 2 == 0:
                    P.op("act", lambda e, o=xb[:, g4 * 512:(g4 + 1) * 512], i=psum[pi]: e.copy(o, i),
                         reads=[psb[pi]], writes=[xbb])
                else:
                    P.op("dve", lambda e, o=xb[:, g4 * 512:(g4 + 1) * 512], i=psum[pi]: e.tensor_copy(o, i),
                         reads=[psb[pi]], writes=[xbb])
            P.dma("pool", out_d[ts * 128:(ts + 1) * 128, :], xb, reads=[xbb], writes=[])

    class RowBufs:
        def __init__(self):
            self.xs = [ar.alloc([128, TT], F32) for _ in range(3)]
            self.xsb = [Buf(f"xs{i}") for i in range(3)]
            self.x2 = [ar.alloc([128, TT], F32) for _ in range(3)]
            self.x2b = [Buf(f"x2{i}") for i in range(3)]
            self.sq = [ar.alloc([128, TT], BF16) for _ in range(2)]
            self.sqb = [Buf(f"sq{i}") for i in range(2)]
            self.rstd = ar.alloc([128, TT], F32)
            self.rstdb = Buf("rstd")
            self.gsc = ar.alloc([128, KC], F32)
            self.gscb = Buf("gsc")
            self.sg = [ar.alloc([128, TT], F32) for _ in range(2)]
            self.sgb = [Buf(f"sg{i}") for i in range(2)]
            self.ny = 0

    def prenorm(tt, li, gi, uT, uTb, psb, rb):
        tok = slice(tt * TT, (tt + 1) * TT)
        ps_s = 7
        xs, xsb, sq, sqb, rstd, rstdb = rb.xs, rb.xsb, rb.sq, rb.sqb, rb.rstd, rb.rstdb
        for kc in range(KC):
            b = kc % len(xs)
            P.dma("sp", xs[b], hT_d[kc][:, tok], reads=[hT_b[tt][kc]], writes=[xsb[b]])
            sb_ = kc % len(sq)
            P.op("act", lambda e, o=sq[sb_], i=xs[b]: e.activation(o, i, AF.Square),
                 reads=[xsb[b]], writes=[sqb[sb_]])
            P.op("pe", lambda e, i=sq[sb_], st=(kc == 0), sp=(kc == KC - 1):
                 e.matmul(psum[ps_s], ones_bf, i, start=st, stop=sp), reads=[sqb[sb_], cb], writes=[psb[ps_s]])
        P.op("act", lambda e: e.activation(rstd, psum[ps_s], AF.Sqrt, bias=eps_t, scale=1.0 / D),
             reads=[psb[ps_s], cb], writes=[rstdb])
        P.op("dve", lambda e: e.reciprocal(rstd, rstd), reads=[rstdb], writes=[rstdb])
        for kc in range(KC):
            b = kc % len(xs)
            P.dma("sp", xs[b], hT_d[kc][:, tok], reads=[hT_b[tt][kc]], writes=[xsb[b]])
            P.op("dve", lambda e, o=uT[:, kc, :], i=xs[b], g=gcol(li, gi, kc):
                 e.scalar_tensor_tensor(o, i, g, rstd, ALU.mult, ALU.mult),
                 reads=[xsb[b], rstdb, cb], writes=[uTb])

    def y_emit(dc, tok, rb, psb, ps_s):
        b = rb.ny % 3
        s_ = rb.ny % 2
        P.op("act", lambda e, o=rb.sq[s_], i=rb.x2[b]: e.activation(o, i, AF.Square),
             reads=[rb.x2b[b]], writes=[rb.sqb[s_]])
        P.op("pe", lambda e, i=rb.sq[s_], st=(dc == 0), sp=(dc == KC - 1):
             e.matmul(psum[ps_s], ones_bf, i, start=st, stop=sp),
             reads=[rb.sqb[s_], cb], writes=[psb[ps_s]])
        P.dma("pool", yT_d[dc][:, tok], rb.x2[b], reads=[rb.x2b[b]], writes=[yT_b[dc]])
        rb.ny += 1

    def y_slot(rb):
        b = rb.ny % 3
        return rb.x2[b], rb.x2b[b]

    def postnorm_residual(tt, li, gi, fac, ps_s, psb, rb):
        tok = slice(tt * TT, (tt + 1) * TT)
        rstd, rstdb, gsc, gscb = rb.rstd, rb.rstdb, rb.gsc, rb.gscb
        ya, yab, ha, hab = rb.x2, rb.x2b, rb.xs, rb.xsb
        P.op("act", lambda e: e.activation(rstd, psum[ps_s], AF.Sqrt, bias=eps_t, scale=1.0 / D),
             reads=[psb[ps_s], cb], writes=[rstdb])
        P.op("dve", lambda e: e.reciprocal(rstd, rstd), reads=[rstdb], writes=[rstdb])
        base = (li * 8 + gi) * KC
        P.op("act", lambda e: e.mul(gsc, gains[:, base:base + KC], float(fac)), reads=[cb], writes=[gscb])
        for dc in range(KC):
            b = dc % 3
            P.dma("pool", ya[b], yT_d[dc][:, tok], reads=[yT_b[dc]], writes=[yab[b]])
            P.dma("pool", ha[b], hT_d[dc][:, tok], reads=[hT_b[tt][dc]], writes=[hab[b]])
            P.op("pool", lambda e, o=ya[b]: e.tensor_tensor(o, o, rstd, ALU.mult),
                 reads=[yab[b], rstdb], writes=[yab[b]])
            P.op("dve", lambda e, o=ha[b], y=ya[b], g=gsc[:, dc:dc + 1]:
                 e.scalar_tensor_tensor(o, y, g, o, ALU.mult, ALU.add),
                 reads=[yab[b], hab[b], gscb], writes=[hab[b]])
            P.dma("pool", hT_d[dc][:, tok], ha[b], reads=[hab[b]], writes=[hT_b[tt][dc]])

    class WStream:
        def __init__(self, nstage=3, nbf=4):
            self.stage = [ar.alloc([128, 2048], F32) for _ in range(nstage)]
            self.stageb = [Buf(f"wst{i}") for i in range(nstage)]
            self.bf = [ar.alloc([128, 2048], BF16) for _ in range(nbf)]
            self.bfb = [Buf(f"wbf{i}") for i in range(nbf)]
            self.i = 0

        def load(self, src_ap, nk, wdt, rows=128):
            i = self.i
            self.i += 1
            s = i % len(self.stage)
            b = i % len(self.bf)
            st = self.stage[s][0:rows, 0:nk * wdt].rearrange("p (a b) -> p a b", b=wdt)
            bf = self.bf[b][0:rows, 0:nk * wdt].rearrange("p (a b) -> p a b", b=wdt)
            P.dma("sp", st, src_ap, writes=[self.stageb[s]])
            if i % 2 == 0:
                P.op("act", lambda e: e.copy(bf, st), reads=[self.stageb[s]], writes=[self.bfb[b]])
            else:
                P.op("dve", lambda e: e.tensor_copy(bf, st), reads=[self.stageb[s]], writes=[self.bfb[b]])
            return bf, self.bfb[b]

    def ffn(li, which):
        psb = phase_reset()
        wg = w[f"ffn{which}_w_gate"][li].rearrange("(k p) n -> p k n", p=128)
        wu = w[f"ffn{which}_w_up"][li].rearrange("(k p) n -> p k n", p=128)
        wd = w[f"ffn{which}_w_down"][li].rearrange("(k p) n -> p k n", p=128)
        gi_pre = 0 if which == 1 else 4
        gi_post = 1 if which == 1 else 5
        uT = ar.alloc([128, KC, TT], BF16)
        uTb = Buf("uT")
        actT = ar.alloc([128, FC, TT], BF16)
        actb = Buf("actT")
        ws = WStream()
        rb = RowBufs()
        sg, sgb = rb.sg, rb.sgb
        for tt in range(ntiles):
            tok = slice(tt * TT, (tt + 1) * TT)
            prenorm(tt, li, gi_pre, uT, uTb, psb, rb)
            for fg in range(FC // 2):
                pb = (fg % 2) * 4
                for pc in range(4):
                    wgp, wgb = ws.load(wg[:, pc * 8:(pc + 1) * 8, fg * 256:(fg + 1) * 256], 8, 256)
                    wup, wub = ws.load(wu[:, pc * 8:(pc + 1) * 8, fg * 256:(fg + 1) * 256], 8, 256)
                    for k8 in range(8):
                        kc = pc * 8 + k8
                        for j in range(2):
                            P.op("pe", lambda e, o=psum[pb + j], l=wgp[:, k8, j * 128:(j + 1) * 128], r=uT[:, kc, :],
                                 st=(kc == 0), sp=(kc == KC - 1): e.matmul(o, l, r, start=st, stop=sp),
                                 reads=[wgb, uTb], writes=[psb[pb + j]])
                        for j in range(2):
                            P.op("pe", lambda e, o=psum[pb + 2 + j], l=wup[:, k8, j * 128:(j + 1) * 128], r=uT[:, kc, :],
                                 st=(kc == 0), sp=(kc == KC - 1): e.matmul(o, l, r, start=st, stop=sp),
                                 reads=[wub, uTb], writes=[psb[pb + 2 + j]])
                for j in range(2):
                    fc = fg * 2 + j
                    s_ = fc % 2
                    P.op("act", lambda e, o=sg[s_], i=psum[pb + j]: e.activation(o, i, AF.Silu),
                         reads=[psb[pb + j]], writes=[sgb[s_]])
                    P.op("dve", lambda e, o=actT[:, fc, :], a=sg[s_], b_=psum[pb + 2 + j]:
                         e.tensor_tensor(o, a, b_, ALU.mult),
                         reads=[sgb[s_], psb[pb + 2 + j]], writes=[actb])
            ps_s = 6
            npc = (FC + 7) // 8
            for dg in range(KC // 2):
                pb = (dg % 2) * 2
                for pc in range(npc):
                    nk = min(8, FC - pc * 8)
                    wdp, wdb = ws.load(wd[:, pc * 8:pc * 8 + nk, dg * 256:(dg + 1) * 256], nk, 256)
                    for k8 in range(nk):
                        fc = pc * 8 + k8
                        for j in range(2):
                            P.op("pe", lambda e, o=psum[pb + j], l=wdp[:, k8, j * 128:(j + 1) * 128], r=actT[:, fc, :],
                                 st=(fc == 0), sp=(fc == FC - 1): e.matmul(o, l, r, start=st, stop=sp),
                                 reads=[wdb, actb], writes=[psb[pb + j]])
                for j in range(2):
                    dc = dg * 2 + j
                    yb, ybb = y_slot(rb)
                    P.op("dve", lambda e, o=yb, i=psum[pb + j]: e.tensor_copy(o, i),
                         reads=[psb[pb + j]], writes=[ybb])
                    y_emit(dc, tok, rb, psb, ps_s)
            postnorm_residual(tt, li, gi_post, 0.5, ps_s, psb, rb)

    def ple(li):
        psb = phase_reset()
        wg = w["ple_w_gate"][li].rearrange("(k p) n -> p k n", p=128)
        wp = w["ple_w_proj"][li].rearrange("(k p) n -> p k n", p=128)
        uT = ar.alloc([128, KC, TT], BF16)
        uTb = Buf("uT")
        pT = ar.alloc([128, 2, TT], BF16)
        pTb = Buf("pT")
        prow = [ar.alloc([128, 256], F32) for _ in range(2)]
        prowb = [Buf("prow0"), Buf("prow1")]
        ws = WStream()
        rb = RowBufs()
        sg, sgb = rb.sg, rb.sgb
        for tt in range(ntiles):
            tok = slice(tt * TT, (tt + 1) * TT)
            prenorm(tt, li, 6, uT, uTb, psb, rb)
            for ts in range(4):
                b = ts % 2
                r0 = tt * TT + ts * 128
                P.dma("sp", prow[b], p_d[li][r0:r0 + 128, :], writes=[prowb[b]])
                for c in range(2):
                    P.op("pe", lambda e, o=psum[7][:, c * 128:(c + 1) * 128], i=prow[b][:, c * 128:(c + 1) * 128]:
                         e.transpose(o, i, ident), reads=[prowb[b], cb], writes=[psb[7]])
                P.op("dve", lambda e, o=pT[:, :, ts * 128:(ts + 1) * 128], i=psum[7][:, 0:256]:
                     e.tensor_copy(o, i.rearrange("p (a b) -> p a b", b=128)), reads=[psb[7]], writes=[pTb])
            ps_s = 6
            for dg in range(KC // 2):
                pb = (dg % 2) * 2
                for pc in range(4):
                    wgp, wgb = ws.load(wg[:, pc * 8:(pc + 1) * 8, dg * 256:(dg + 1) * 256], 8, 256)
                    for k8 in range(8):
                        kc = pc * 8 + k8
                        for j in range(2):
                            P.op("pe", lambda e, o=psum[pb + j], l=wgp[:, k8, j * 128:(j + 1) * 128], r=uT[:, kc, :],
                                 st=(kc == 0), sp=(kc == KC - 1): e.matmul(o, l, r, start=st, stop=sp),
                                 reads=[wgb, uTb], writes=[psb[pb + j]])
                wpp, wpb = ws.load(wp[:, 0:2, dg * 256:(dg + 1) * 256], 2, 256)
                for kc in range(2):
                    for j in range(2):
                        P.op("pe", lambda e, o=psum[4 + j], l=wpp[:, kc, j * 128:(j + 1) * 128], r=pT[:, kc, :],
                             st=(kc == 0), sp=(kc == 1): e.matmul(o, l, r, start=st, stop=sp),
                             reads=[wpb, pTb], writes=[psb[4 + j]])
                for j in range(2):
                    dc = dg * 2 + j
                    s_ = dc % 2
                    P.op("act", lambda e, o=sg[s_], i=psum[pb + j]: e.activation(o, i, AF.Sigmoid),
                         reads=[psb[pb + j]], writes=[sgb[s_]])
                    yb, ybb = y_slot(rb)
                    P.op("dve", lambda e, o=yb, a=sg[s_], b_=psum[4 + j]: e.tensor_tensor(o, a, b_, ALU.mult),
                         reads=[sgb[s_], psb[4 + j]], writes=[ybb])
                    y_emit(dc, tok, rb, psb, ps_s)
            postnorm_residual(tt, li, 7, 1.0, ps_s, psb, rb)

    def w_in_phase(li):
        psb = phase_reset()
        wi = w["w_in"][li].rearrange("(k p) n -> p k n", p=128)
        uT = ar.alloc([128, KC, TT], BF16)
        uTb = Buf("uT")
        ws = WStream()
        rb = RowBufs()
        rope = ar.alloc([128, 4, TT], F32)
        ropeb = Buf("rope")
        z32 = [ar.alloc([128, TT], F32) for _ in range(2)]
        z32b = [Buf("z32_0"), Buf("z32_1")]
        t1 = [ar.alloc([128, TT], F32) for _ in range(2)]
        t1b = [Buf("t1_0"), Buf("t1_1")]
        t2 = [ar.alloc([128, TT], F32) for _ in range(2)]
        t2b = [Buf("t2_0"), Buf("t2_1")]
        zo = [ar.alloc([128, TT], BF16) for _ in range(3)]
        zob = [Buf(f"zo{i}") for i in range(3)]
        vo = [ar.alloc([128, 512], BF16) for _ in range(3)]
        vob = [Buf(f"vo{i}") for i in range(3)]
        fo = [ar.alloc([128, 8], F32) for _ in range(2)]
        fob = [Buf("fo0"), Buf("fo1")]
        go = ar.alloc([128, TT], F32)
        gob = Buf("go")
        cnt = {"z": 0, "r": 0, "v": 0, "f": 0}
        for tt in range(ntiles):
            tok = slice(tt * TT, (tt + 1) * TT)
            prenorm(tt, li, 2, uT, uTb, psb, rb)
            P.dma("sp", rope, rope_d.ap().rearrange("r p t -> p r t")[:, :, tok], writes=[ropeb])
            groups = []
            for (nm, c0, nch, rt) in FM_JOBS:
                i = 0
                while i < nch:
                    n = min(2, nch - i)
                    groups.append((c0 + i * 128, n * 128, [(ZQK[nm] + i + j, rt) for j in range(n)]))
                    i += n
            groups.append((GC_COL, 24, [("gc", 0)]))
            for gidx, (c0, wdt, chunks) in enumerate(groups):
                pb = (gidx % 2) * 2
                for pc in range(4):
                    wp_, wpb_ = ws.load(wi[:, pc * 8:(pc + 1) * 8, c0:c0 + wdt], 8, wdt)
                    for k8 in range(8):
                        kc = pc * 8 + k8
                        for j, (zi, rt) in enumerate(chunks):
                            m = 24 if zi == "gc" else 128
                            P.op("pe", lambda e, o=psum[pb + j][0:m, :], l=wp_[:, k8, j * 128:j * 128 + m], r=uT[:, kc, :],
                                 st=(kc == 0), sp=(kc == KC - 1): e.matmul(o, l, r, start=st, stop=sp),
                                 reads=[wpb_, uTb], writes=[psb[pb + j]])
                for j, (zi, rt) in enumerate(chunks):
                    if zi == "gc":
                        P.op("act", lambda e, i=psum[pb + j][0:24, :]: e.activation(go[0:24, :], i, AF.Sigmoid),
                             reads=[psb[pb + j]], writes=[gob])
                        P.dma("pool", gT_d[:, tok], go[0:24, :], reads=[gob], writes=[gT_b])
                        continue
                    zb = cnt["z"] % 3
                    cnt["z"] += 1
                    if rt == 0:
                        P.op("act", lambda e, o=zo[zb], i=psum[pb + j]: e.copy(o, i),
                             reads=[psb[pb + j]], writes=[zob[zb]])
                    else:
                        r_ = cnt["r"] % 2
                        cnt["r"] += 1
                        pr = 4 + r_
                        ci = 0 if rt == 64 else 2
                        rtm = consts[:, C_RT64:C_RT64 + 128] if rt == 64 else consts[:, C_RT32:C_RT32 + 128]
                        P.op("act", lambda e, o=z32[r_], i=psum[pb + j]: e.copy(o, i),
                             reads=[psb[pb + j]], writes=[z32b[r_]])
                        P.op("pe", lambda e, o=psum[pr], l=rtm, r=z32[r_]: e.matmul(o, l, r, start=True, stop=True),
                             reads=[z32b[r_], cb], writes=[psb[pr]])
                        P.op("pool", lambda e, o=t1[r_], a=z32[r_], b_=rope[:, ci, :]: e.tensor_tensor(o, a, b_, ALU.mult),
                             reads=[z32b[r_], ropeb], writes=[t1b[r_]])
                        P.op("dve", lambda e, o=t2[r_], a=psum[pr], b_=rope[:, ci + 1, :]: e.tensor_tensor(o, a, b_, ALU.mult),
                             reads=[psb[pr], ropeb], writes=[t2b[r_]])
                        P.op("pool", lambda e, o=zo[zb], a=t1[r_], b_=t2[r_]: e.tensor_tensor(o, a, b_, ALU.add),
                             reads=[t1b[r_], t2b[r_]], writes=[zob[zb]])
                    P.dma("pool", zqk_d[zi][:, tok], zo[zb], reads=[zob[zb]], writes=[zqk_b[zi]])
            for (c0, ncol, v0, nf) in TM_JOBS:
                for pc in range(8):
                    wp_, wpb_ = ws.load(wi[:, pc * 4:(pc + 1) * 4, c0:c0 + ncol], 4, ncol)
                    for k4 in range(4):
                        kc = pc * 4 + k4
                        for ts in range(4):
                            P.op("pe", lambda e, o=psum[ts][:, 0:ncol], l=uT[:, kc, ts * 128:(ts + 1) * 128], r=wp_[:, k4, :],
                                 st=(kc == 0), sp=(kc == KC - 1): e.matmul(o, l, r, start=st, stop=sp),
                                 reads=[wpb_, uTb], writes=[psb[ts]])
                nv = ncol - nf
                for ts in range(4):
                    vb_ = cnt["v"] % 3
                    cnt["v"] += 1
                    r0 = tt * TT + ts * 128
                    if ts % 2 == 0:
                        P.op("act", lambda e, o=vo[vb_][:, 0:nv], i=psum[ts][:, 0:nv]: e.copy(o, i),
                             reads=[psb[ts]], writes=[vob[vb_]])
                    else:
                        P.op("dve", lambda e, o=vo[vb_][:, 0:nv], i=psum[ts][:, 0:nv]: e.tensor_copy(o, i),
                             reads=[psb[ts]], writes=[vob[vb_]])
                    P.dma("pool", zv_d[r0:r0 + 128, v0:v0 + nv], vo[vb_][:, 0:nv], reads=[vob[vb_]], writes=[zv_b])
                    if nf:
                        f_ = cnt["f"] % 2
                        cnt["f"] += 1
                        eng = "act" if ts % 2 == 0 else "dve"
                        if eng == "act":
                            P.op("act", lambda e, o=fo[f_][:, 0:nf], i=psum[ts][:, nv:ncol]: e.copy(o, i),
                                 reads=[psb[ts]], writes=[fob[f_]])
                        else:
                            P.op("dve", lambda e, o=fo[f_][:, 0:nf], i=psum[ts][:, nv:ncol]: e.tensor_copy(o, i),
                                 reads=[psb[ts]], writes=[fob[f_]])
                        P.dma("pool", zf_d[r0:r0 + 128, 0:nf], fo[f_][:, 0:nf], reads=[fob[f_]], writes=[zf_b])

    def w_out_phase(li):
        psb = phase_reset()
        wo = w["w_out"][li].rearrange("(k p) n -> p k n", p=128)
        oT = ar.alloc([128, 26, TT], BF16)
        oTb = Buf("oTs")
        ws = WStream()
        rb = RowBufs()
        for tt in range(ntiles):
            tok = slice(tt * TT, (tt + 1) * TT)
            for oc in range(26):
                P.dma("sp", oT[:, oc, :], oT_d[oc][:, tok], reads=[oT_b[oc]], writes=[oTb])
            ps_s = 6
            for dg in range(KC // 2):
                pb = (dg % 2) * 2
                for pc in range(4):
                    nk = min(8, 26 - pc * 8)
                    wp_, wpb_ = ws.load(wo[:, pc * 8:pc * 8 + nk, dg * 256:(dg + 1) * 256], nk, 256)
                    for k8 in range(nk):
                        oc = pc * 8 + k8
                        for j in range(2):
                            P.op("pe", lambda e, o=psum[pb + j], l=wp_[:, k8, j * 128:(j + 1) * 128], r=oT[:, oc, :],
                                 st=(oc == 0), sp=(oc == 25): e.matmul(o, l, r, start=st, stop=sp),
                                 reads=[wpb_, oTb], writes=[psb[pb + j]])
                for j in range(2):
                    dc = dg * 2 + j
                    yb, ybb = y_slot(rb)
                    P.op("dve", lambda e, o=yb, i=psum[pb + j]: e.tensor_copy(o, i),
                         reads=[psb[pb + j]], writes=[ybb])
                    y_emit(dc, tok, rb, psb, ps_s)
            postnorm_residual(tt, li, 3, 1.0, ps_s, psb, rb)

    ATTN = {}
    exec_attn = None
    def attention_phase(li):
        psb = phase_reset()
        lam_init = 0.8 - 0.6 * float(np.exp(-0.3 * li))
        masks = ar.alloc([128, NMASK], BF16)
        mb = Buf("masks")
        P.dma("sp", masks, masks_d.ap(), writes=[mb])

        def Mc(j, n=512):
            return masks[:, M_CAUS + j * 512:M_CAUS + j * 512 + n]

        def Mw(j):
            return masks[:, M_WIN + j * 512:M_WIN + (j + 1) * 512]

        Umask = masks[:, M_U:M_U + 128]
        NB = 2
        hq = [ar.alloc([128, T], BF16) for _ in range(NB)]
        hqb = [Buf(f"hq{i}") for i in range(NB)]
        hk = [ar.alloc([128, T], BF16) for _ in range(NB)]
        hkb = [Buf(f"hk{i}") for i in range(NB)]
        hv = [ar.alloc([128, 16, 128], BF16) for _ in range(NB)]
        hvb = [Buf(f"hv{i}") for i in range(NB)]
        pt = [ar.alloc([128, 512], BF16) for _ in range(3)]
        ptb = [Buf(f"pt{i}") for i in range(3)]
        rec = [ar.alloc([128, 512], F32) for _ in range(2)]
        recb = [Buf("rec0"), Buf("rec1")]
        tq = [ar.alloc([128, 512], F32) for _ in range(3)]
        tqb = [Buf(f"tq{i}") for i in range(3)]
        ob = [ar.alloc([128, 512], BF16) for _ in range(2)]
        obb = [Buf("ob0"), Buf("ob1")]
        sqa = ar.alloc([128, 512], BF16)
        sqab = Buf("sqa")
        cnt = {"pt": 0, "ld": 0, "ob": 0, "mk": 0}
        zvr = zv_d.ap().rearrange("(kb p) c -> p kb c", p=128)

        def load_head(qi, ki, vcol):
            s = cnt["ld"] % NB
            cnt["ld"] += 1
            if qi is not None:
                P.dma("sp", hq[s], zqk_d[qi], reads=[zqk_b[qi]], writes=[hqb[s]])
            if ki is not None:
                P.dma("sp", hk[s], zqk_d[ki], reads=[zqk_b[ki]], writes=[hkb[s]])
            if vcol is not None:
                P.dma("sp", hv[s], zvr[:, :, vcol:vcol + 128], reads=[zv_b], writes=[hvb[s]])
            return s

        def flash(pairs, qap, qbufs, nq, scale, oi, di):
            n = len(pairs)
            slots = [None] * n

            def S(i):
                p = pairs[i]
                nk = p.get("nk", 128)
                P.op("pe", lambda e, o=psum[i % 2][0:nk, 0:nq], l=p["k"], r=qap: e.matmul(o, l, r, start=True, stop=True),
                     reads=p["kb"] + qbufs, writes=[psb[i % 2]])

            def E(i):
                p = pairs[i]
                nk = p.get("nk", 128)
                s = cnt["pt"] % 3
                cnt["pt"] += 1
                slots[i] = s
                if "expfn" in p:
                    p["expfn"](psum[i % 2], psb[i % 2], pt[s], ptb[s])
                    return
                P.op("act", lambda e, o=pt[s][0:nk, 0:nq], i_=psum[i % 2][0:nk, 0:nq]: e.activation(o, i_, AF.Exp, scale=scale),
                     reads=[psb[i % 2]], writes=[ptb[s]])
                if p.get("mask") is not None:
                    eng = "dve" if cnt["mk"] % 2 == 0 else "pool"
                    cnt["mk"] += 1
                    P.op(eng, lambda e, o=pt[s][0:nk, 0:nq], m=p["mask"]: e.tensor_tensor(o, o, m, ALU.mult),
                         reads=[ptb[s]] + p["maskb"], writes=[ptb[s]])

            def OV(i):
                p = pairs[i]
                nk = p.get("nk", 128)
                s = slots[i]
                P.op("pe", lambda e, o=psum[oi][:, 0:nq], l=p["v"], r=pt[s][0:nk, 0:nq], st=(i == 0), sp=(i == n - 1):
                     e.matmul(o, l, r, start=st, stop=sp), reads=p["vb"] + [ptb[s]], writes=[psb[oi]])
                if di is not None:
                    P.op("pe", lambda e, o=psum[di][:, 0:nq], l=ones_bf[0:nk, :], r=pt[s][0:nk, 0:nq], st=(i == 0), sp=(i == n - 1):
                         e.matmul(o, l, r, start=st, stop=sp), reads=[ptb[s], cb], writes=[psb[di]])

            S(0)
            for i in range(n):
                if i + 1 < n:
                    S(i + 1)
                E(i)
                OV(i)

        def normalize(oi, di, nq, dst, dstb, r_):
            P.op("dve", lambda e, o=rec[r_][:, 0:nq], i_=psum[di][:, 0:nq]: e.tensor_scalar(o, i_, 1e-30, None, ALU.max),
                 reads=[psb[di]], writes=[recb[r_]])
            P.op("dve", lambda e, o=rec[r_][:, 0:nq]: e.reciprocal(o, o), reads=[recb[r_]], writes=[recb[r_]])
            P.op("dve", lambda e, o=dst, a=psum[oi][:, 0:nq], b_=rec[r_][:, 0:nq]: e.tensor_tensor(o, a, b_, ALU.mult),
                 reads=[psb[oi], recb[r_]], writes=[dstb])

        def store_o(idx, cols, src, srcb):
            P.dma("pool", oT_d[idx][:, cols], src, reads=[srcb], writes=[oT_b[idx]])

        def bc_row(tensor, off, n):
            return AP(tensor, off, [[0, 128], [1, n]])

        def diff_attn():
            lv = ar.alloc([128, 4, 64], F32)
            lvb = Buf("lv")
            for i, nm in enumerate(["diff_lam_q1", "diff_lam_k1", "diff_lam_q2", "diff_lam_k2"]):
                P.dma("sp", lv[:, i, :], bc_row(w[nm], li * 64, 64), writes=[lvb])
            sm = ar.alloc([128, 8], F32)
            smb = Buf("sm")
            gsub = ar.alloc([128, 2], F32)
            gsubb = Buf("gsub")
            P.dma("sp", gsub[:, 0:1], AP(w["diff_subln_g"], li * 128, [[1, 128], [1, 1]]), writes=[gsubb])
            P.op("dve", lambda e: e.tensor_tensor(lv[:, 0, :], lv[:, 0, :], lv[:, 1, :], ALU.mult), reads=[lvb], writes=[lvb])
            P.op("dve", lambda e: e.tensor_tensor(lv[:, 2, :], lv[:, 2, :], lv[:, 3, :], ALU.mult), reads=[lvb], writes=[lvb])
            P.op("dve", lambda e: e.reduce_sum(sm[:, 0:1], lv[:, 0, :], AX.X), reads=[lvb], writes=[smb])
            P.op("dve", lambda e: e.reduce_sum(sm[:, 1:2], lv[:, 2, :], AX.X), reads=[lvb], writes=[smb])
            P.op("act", lambda e: e.activation(sm[:, 2:4], sm[:, 0:2], AF.Exp), reads=[smb], writes=[smb])
            P.op("dve", lambda e: e.tensor_tensor(sm[:, 4:5], sm[:, 3:4], sm[:, 2:3], ALU.subtract), reads=[smb], writes=[smb])
            P.op("dve", lambda e: e.tensor_scalar(sm[:, 5:6], sm[:, 4:5], -lam_init, None, ALU.add), reads=[smb], writes=[smb])
            P.op("dve", lambda e: e.tensor_scalar(gsub[:, 1:2], gsub[:, 0:1], 1.0 - lam_init, None, ALU.mult),
                 reads=[gsubb], writes=[gsubb])
            neglam = sm[:, 5:6]
            for h in (range(8) if heads is None else heads.get("a", [])):
                s = load_head(ZQK["qa"] + h, ZQK["ka"] + h, ZV["va"] + h * 128)
                for qb in range(4):
                    for c in range(2):
                        pairs = []
                        for kb in range(4 * qb + 4):
                            d = {"k": hk[s][c * 64:(c + 1) * 64, kb * 128:(kb + 1) * 128], "kb": [hkb[s]],
                                 "v": hv[s][:, kb, :], "vb": [hvb[s]]}
                            if kb >= 4 * qb:
                                d["mask"] = Mc(kb - 4 * qb)
                                d["maskb"] = [mb]
                            pairs.append(d)
                        flash(pairs, hq[s][c * 64:(c + 1) * 64, qb * 512:(qb + 1) * 512], [hqb[s]], 512, 0.125, 2 + c, 4 + c)
                    normalize(2, 4, 512, tq[0], tqb[0], 0)
                    normalize(3, 5, 512, tq[1], tqb[1], 1)
                    P.op("dve", lambda e: e.scalar_tensor_tensor(tq[2], tq[1], neglam, tq[0], ALU.mult, ALU.add),
                         reads=[tqb[0], tqb[1], smb], writes=[tqb[2]])
                    P.op("act", lambda e: e.activation(sqa, tq[2], AF.Square), reads=[tqb[2]], writes=[sqab])
                    P.op("pe", lambda e: e.matmul(psum[6], ones_bf, sqa, start=True, stop=True),
                         reads=[sqab, cb], writes=[psb[6]])
                    P.op("act", lambda e: e.activation(rec[0], psum[6], AF.Sqrt, bias=eps_t, scale=1.0 / 128),
                         reads=[psb[6], cb], writes=[recb[0]])
                    P.op("dve", lambda e: e.reciprocal(rec[0], rec[0]), reads=[recb[0]], writes=[recb[0]])
                    o_ = cnt["ob"] % 2
                    cnt["ob"] += 1
                    P.op("dve", lambda e, o=ob[o_]: e.scalar_tensor_tensor(o, tq[2], gsub[:, 1:2], rec[0], ALU.mult, ALU.mult),
                         reads=[tqb[2], gsubb, recb[0]], writes=[obb[o_]])
                    store_o(h, slice(qb * 512, (qb + 1) * 512), ob[o_], obb[o_])

        def fox_attn():
            zf = ar.alloc([128, 16, 8], F32)
            zfb = Buf("zfs")
            bfb = ar.alloc([128, 8], F32)
            bfbb = Buf("bfb")
            cl = ar.alloc([128, 16, 8], F32)
            clb = Buf("cl")
            cref = ar.alloc([128, 16, 8], F32)
            crefb = Buf("cref")
            tot = ar.alloc([128, 16, 8], F32)
            totb = Buf("tot")
            bias4 = [ar.alloc([128, 4], F32) for _ in range(3)]
            bias4b = [Buf(f"b4{i}") for i in range(3)]
            P.dma("sp", zf, zf_d.ap().rearrange("(kb p) c -> p kb c", p=128), reads=[zf_b], writes=[zfb])
            P.op("dve", lambda e: e.memset(bfb, 0.0), writes=[bfbb])
            P.dma("sp", bfb[:, 0:7], bc_row(w["fox_bf"], li * 7, 7), writes=[bfbb])
            P.op("dve", lambda e: e.tensor_tensor(zf, zf, bcast_free(bfb, 16, 1), ALU.add), reads=[zfb, bfbb], writes=[zfb])
            P.op("act", lambda e: e.activation(zf, zf, AF.Exp, scale=-1.0), reads=[zfb], writes=[zfb])
            P.op("act", lambda e: e.activation(zf, zf, AF.Ln, bias=one_t, scale=1.0), reads=[zfb, cb], writes=[zfb])
            zf2 = zf.rearrange("p a b -> p (a b)")
            P.op("pe", lambda e: e.matmul(psum[6][:, 0:128], ones_f, zf2, start=True, stop=True), reads=[zfb, cb], writes=[psb[6]])
            P.op("dve", lambda e: e.tensor_copy(tot.rearrange("p a b -> p (a b)"), psum[6][:, 0:128]), reads=[psb[6]], writes=[totb])
            P.op("dve", lambda e: e.memset(cref[:, 0, :], 0.0), writes=[crefb])
            for b in range(1, 16):
                P.op("dve", lambda e, b=b: e.tensor_tensor(cref[:, b, :], cref[:, b - 1, :], tot[:, b - 1, :], ALU.add),
                     reads=[crefb, totb], writes=[crefb])
            P.op("pe", lambda e: e.matmul(psum[6][:, 0:128], consts[:, C_TRIU:C_TRIU + 128], zf2, start=True, stop=True),
                 reads=[zfb, cb], writes=[psb[6]])
            P.op("dve", lambda e: e.tensor_tensor(cl.rearrange("p a b -> p (a b)"), psum[6][:, 0:128],
                                                  cref.rearrange("p a b -> p (a b)"), ALU.add),
                 reads=[psb[6], crefb], writes=[clb])
            P.op("pe", lambda e: e.matmul(psum[6][:, 0:128], consts[:, C_SEL64:C_SEL64 + 128],
                                          cl.rearrange("p a b -> p (a b)"), start=True, stop=True),
                 reads=[clb, cb], writes=[psb[6]])
            P.op("dve", lambda e: e.tensor_copy(cref.rearrange("p a b -> p (a b)"), psum[6][:, 0:128]), reads=[psb[6]], writes=[crefb])
            sc = 128.0 ** -0.5
            bc = {"i": 0}
            for h in (range(7) if heads is None else heads.get("b", [])):
                s = load_head(ZQK["qb"] + h, ZQK["kb"] + h, ZV["vb"] + h * 128)
                for qb in range(4):
                    pairs = []
                    for kb in range(4 * qb + 4):
                        def expfn(ps, psbuf, ptile, ptbuf, kb=kb, qb=qb, h=h):
                            b4 = bc["i"] % 3
                            bc["i"] += 1
                            P.op("dve", lambda e, o=bias4[b4]: e.tensor_scalar(o, cref[:, 4 * qb:4 * qb + 4, h], -1.0, cl[:, kb, h:h + 1],
                                                                               ALU.mult, ALU.add),
                                 reads=[crefb, clb], writes=[bias4b[b4]])
                            m = kb - 4 * qb
                            for j in range(4):
                                cs = slice(j * 128, (j + 1) * 128)
                                if m > j:
                                    P.op("pool", lambda e, o=ptile[:, cs]: e.memset(o, 0.0), reads=[], writes=[ptbuf])
                                    continue
                                P.op("act", lambda e, o=ptile[:, cs], i_=ps[:, cs], bcol=bias4[b4][:, j:j + 1]:
                                     e.activation(o, i_, AF.Exp, bias=bcol, scale=sc),
                                     reads=[psbuf, bias4b[b4]], writes=[ptbuf])
                                if m == j:
                                    P.op("dve", lambda e, o=ptile[:, cs]: e.tensor_tensor(o, o, Mc(0, 128), ALU.mult),
                                         reads=[ptbuf, mb], writes=[ptbuf])
                        pairs.append({"k": hk[s][:, kb * 128:(kb + 1) * 128], "kb": [hkb[s]], "v": hv[s][:, kb, :], "vb": [hvb[s]],
                                      "expfn": expfn})
                    flash(pairs, hq[s][:, qb * 512:(qb + 1) * 512], [hqb[s]], 512, sc, 2 + qb % 2, 4 + qb % 2)
                    o_ = cnt["ob"] % 2
                    cnt["ob"] += 1
                    normalize(2 + qb % 2, 4 + qb % 2, 512, ob[o_], obb[o_], qb % 2)
                    store_o(8 + h, slice(qb * 512, (qb + 1) * 512), ob[o_], obb[o_])

        def nsa_attn():
            sc = 128.0 ** -0.5
            w1s = ar.alloc([128, 32, 128], F32)
            w1sb = Buf("w1s")
            w1 = ar.alloc([128, 32, 128], BF16)
            w1b = Buf("w1")
            w2s = ar.alloc([128, 128], F32)
            w2sb = Buf("w2s")
            w2 = ar.alloc([128, 128], BF16)
            w2b = Buf("w2")
            pes = ar.alloc([32, 128], F32)
            pesb = Buf("pes")
            peT = ar.alloc([128, 32], BF16)
            peTb = Buf("peT")
            bcol = ar.alloc([128, 1], F32)
            bcolb = Buf("bcol")
            hx = ar.alloc([128, 128], F32)
            hxb = Buf("hx")
            hy = ar.alloc([128, 128], F32)
            hyb = Buf("hy")
            hact = ar.alloc([128, 128], BF16)
            hactb = Buf("hact")
            kcT = [ar.alloc([128, 128], BF16) for _ in range(2)]
            kcTb = [Buf("kcT0"), Buf("kcT1")]
            vc = [ar.alloc([128, 128], BF16) for _ in range(2)]
            vcb = [Buf("vc0"), Buf("vc1")]
            src = ar.alloc([128, T], BF16)
            srcb = Buf("src")
            cmpm = masks[0:127, M_CMP:M_CMP + T]
            emat = masks[0:32, M_EXP:M_EXP + T]
            ovl = consts[0:127, C_OVL:C_OVL + 32]

            def compress(which, g):
                pe_d = w["nsa_pe_k" if which == "k" else "nsa_pe_v"]
                w1_d = w["nsa_wk1" if which == "k" else "nsa_wv1"]
                w2_d = w["nsa_wk2" if which == "k" else "nsa_wv2"]
                zi = (ZQK["kcc"] if which == "k" else ZQK["vcc"]) + g
                P.dma("sp", w1s, w1_d[li].rearrange("(l d) m -> d l m", d=128), writes=[w1sb])
                P.op("act", lambda e: e.copy(w1, w1s), reads=[w1sb], writes=[w1b])
                P.dma("sp", w2s, w2_d[li], writes=[w2sb])
                P.op("dve", lambda e: e.tensor_copy(w2, w2s), reads=[w2sb], writes=[w2b])
                P.dma("sp", pes, pe_d[li], writes=[pesb])
                P.op("pe", lambda e: e.transpose(psum[6][:, 0:32], pes, ident[0:32, 0:32]), reads=[pesb, cb], writes=[psb[6]])
                P.op("dve", lambda e: e.tensor_copy(peT, psum[6][:, 0:32]), reads=[psb[6]], writes=[peTb])
                P.dma("sp", src, zqk_d[zi], reads=[zqk_b[zi]], writes=[srcb])
                for l in range(32):
                    P.op("pe", lambda e, l=l: e.matmul(psum[7][:, 0:1], w1[:, l, :], peT[:, l:l + 1], start=(l == 0), stop=(l == 31)),
                         reads=[w1b, peTb], writes=[psb[7]])
                P.op("dve", lambda e: e.tensor_copy(bcol, psum[7][:, 0:1]), reads=[psb[7]], writes=[bcolb])
                for l in range(32):
                    P.op("pe", lambda e, l=l: e.matmul(psum[6][:, 0:127], w1[:, l, :], src[:, l:l + 2017:16], start=(l == 0), stop=(l == 31)),
                         reads=[w1b, srcb], writes=[psb[6]])
                P.op("act", lambda e: e.activation(hx[:, 0:127], psum[6][:, 0:127], AF.Identity, bias=bcol, scale=1.0),
                     reads=[psb[6], bcolb], writes=[hxb])
                P.op("dve", lambda e: e.tensor_tensor(hy[:, 0:127], hx[:, 0:127], hx[:, 0:127], ALU.mult), reads=[hxb], writes=[hyb])
                P.op("dve", lambda e: e.tensor_scalar(hy[:, 0:127], hy[:, 0:127], 0.044715, 1.0, ALU.mult, ALU.add), reads=[hyb], writes=[hyb])
                P.op("dve", lambda e: e.tensor_tensor(hy[:, 0:127], hy[:, 0:127], hx[:, 0:127], ALU.mult), reads=[hyb, hxb], writes=[hyb])
                P.op("act", lambda e: e.activation(hy[:, 0:127], hy[:, 0:127], AF.Sigmoid, scale=1.5957691216057308), reads=[hyb], writes=[hyb])
                P.op("dve", lambda e: e.tensor_tensor(hact[:, 0:127], hy[:, 0:127], hx[:, 0:127], ALU.mult), reads=[hyb, hxb], writes=[hactb])
                if which == "k":
                    P.op("pe", lambda e: e.matmul(psum[7][:, 0:127], w2, hact[:, 0:127], start=True, stop=True),
                         reads=[w2b, hactb], writes=[psb[7]])
                    P.op("dve", lambda e: e.tensor_copy(kcT[g][:, 0:127], psum[7][:, 0:127]), reads=[psb[7]], writes=[kcTb[g]])
                else:
                    P.op("pe", lambda e: e.matmul(psum[7][0:127, 0:128], hact[:, 0:127], w2, start=True, stop=True),
                         reads=[w2b, hactb], writes=[psb[7]])
                    P.op("dve", lambda e: e.tensor_copy(vc[g][0:127, :], psum[7][0:127, 0:128]), reads=[psb[7]], writes=[vcb[g]])

            for g in range(2):
                compress("k", g)
                compress("v", g)

            pn = [ar.alloc([128, 512], F32) for _ in range(4)]
            pnb = [Buf(f"pn{i}") for i in range(4)]
            pnbf = [ar.alloc([128, 512], BF16) for _ in range(2)]
            pnbfb = [Buf("pnbf0"), Buf("pnbf1")]
            ocmp = [ar.alloc([128, 512], F32) for _ in range(4)]
            ocmpb = [Buf(f"ocmp{i}") for i in range(4)]
            acc = ar.alloc([128, 512], F32)
            accb = Buf("acc")
            imp = ar.alloc([128, 32], F32)
            impb = Buf("imp")
            cmp3 = ar.alloc([128, 32, 32], F32)
            cmp3b = Buf("cmp3")
            cntt = ar.alloc([128, 32], F32)
            cnttb = Buf("cntt")
            selT = ar.alloc([32, 512], BF16)
            selTb = Buf("selT")
            msk = ar.alloc([128, 16, 512], BF16)
            mskb = [Buf(f"msk{i}") for i in range(16)]
            gbc = ar.alloc([128, 3, 512], F32)
            gbcb = Buf("gbc")
            hq4 = [ar.alloc([128, T], BF16) for _ in range(4)]
            hq4b = [Buf(f"hq4{i}") for i in range(4)]
            ksel = ar.alloc([128, T], BF16)
            kselb = Buf("ksel")
            kwin = ar.alloc([128, T], BF16)
            kwinb = Buf("kwin")
            vsel = ar.alloc([128, 16, 128], BF16)
            vselb = Buf("vsel")
            vwin = ar.alloc([128, 16, 128], BF16)
            vwinb = Buf("vwin")
            atab = consts[:, C_ATAB:C_ATAB + 512].rearrange("p (a b) -> p a b", b=32)
            for g in (range(2) if heads is None else heads.get("c", [])):
                for r in range(4):
                    qi = ZQK["qc"] + g * 4 + r
                    P.dma("sp", hq4[r], zqk_d[qi], reads=[zqk_b[qi]], writes=[hq4b[r]])
                P.dma("sp", ksel, zqk_d[ZQK["ksc"] + g], reads=[zqk_b[ZQK["ksc"] + g]], writes=[kselb])
                P.dma("sp", kwin, zqk_d[ZQK["kwc"] + g], reads=[zqk_b[ZQK["kwc"] + g]], writes=[kwinb])
                P.dma("sp", vsel, zvr[:, :, ZV["vsc"] + g * 128:ZV["vsc"] + (g + 1) * 128], reads=[zv_b], writes=[vselb])
                P.dma("sp", vwin, zvr[:, :, ZV["vwc"] + g * 128:ZV["vwc"] + (g + 1) * 128], reads=[zv_b], writes=[vwinb])
                for qb in range(4):
                    qs = slice(qb * 512, (qb + 1) * 512)
                    for r in range(4):
                        P.op("pe", lambda e, r=r: e.matmul(psum[0][0:127, :], kcT[g][:, 0:127], hq4[r][:, qs], start=True, stop=True),
                             reads=[kcTb[g], hq4b[r]], writes=[psb[0]])
                        P.op("act", lambda e, r=r: e.activation(pn[r][0:127, :], psum[0][0:127, :], AF.Exp, scale=sc),
                             reads=[psb[0]], writes=[pnb[r]])
                        P.op("dve", lambda e, r=r: e.tensor_tensor(pn[r][0:127, :], pn[r][0:127, :], cmpm[:, qs], ALU.mult),
                             reads=[pnb[r], mb], writes=[pnb[r]])
                        P.op("pe", lambda e, r=r: e.matmul(psum[4], ones_f[0:127, :], pn[r][0:127, :], start=True, stop=True),
                             reads=[pnb[r], cb], writes=[psb[4]])
                        P.op("dve", lambda e: e.tensor_scalar(rec[0], psum[4], 1e-30, None, ALU.max), reads=[psb[4]], writes=[recb[0]])
                        P.op("dve", lambda e: e.reciprocal(rec[0], rec[0]), reads=[recb[0]], writes=[recb[0]])
                        P.op("dve", lambda e, r=r: e.tensor_tensor(pn[r][0:127, :], pn[r][0:127, :], rec[0][0:127, :], ALU.mult),
                             reads=[pnb[r], recb[0]], writes=[pnb[r]])
                        b2 = r % 2
                        P.op("pool", lambda e, r=r, b2=b2: e.tensor_copy(pnbf[b2][0:127, :], pn[r][0:127, :]),
                             reads=[pnb[r]], writes=[pnbfb[b2]])
                        P.op("pe", lambda e, b2=b2: e.matmul(psum[2], vc[g][0:127, :], pnbf[b2][0:127, :], start=True, stop=True),
                             reads=[vcb[g], pnbfb[b2]], writes=[psb[2]])
                        P.op("act", lambda e, r=r: e.copy(ocmp[r], psum[2]), reads=[psb[2]], writes=[ocmpb[r]])
                    for st_ in range(4):
                        tcols = slice(st_ * 128, (st_ + 1) * 128)
                        for r in range(4):
                            P.op("pe", lambda e, r=r: e.matmul(psum[6][:, 0:32], pn[r][0:127, tcols], ovl, start=(r == 0), stop=(r == 3)),
                                 reads=[pnb[r], cb], writes=[psb[6]])
                        P.op("dve", lambda e: e.tensor_tensor(imp, psum[6][:, 0:32], atab[:, qb * 4 + st_, :], ALU.add),
                             reads=[psb[6], cb], writes=[impb])
                        P.op("dve", lambda e: e.tensor_tensor(cmp3, bcast_free(imp, 32, 1), bcast_free(imp, 32, 2), ALU.is_gt),
                             reads=[impb], writes=[cmp3b])
                        P.op("dve", lambda e: e.reduce_sum(cntt, cmp3, AX.X), reads=[cmp3b], writes=[cnttb])
                        P.op("dve", lambda e: e.tensor_scalar(cntt, cntt, 15.5, None, ALU.is_lt), reads=[cnttb], writes=[cnttb])
                        P.op("pe", lambda e: e.transpose(psum[7][0:32, 0:128], cntt, ident), reads=[cnttb, cb], writes=[psb[7]])
                        P.op("act", lambda e: e.copy(selT[:, tcols], psum[7][0:32, 0:128]), reads=[psb[7]], writes=[selTb])
                    nkb = 4 * qb + 4
                    for kb in range(nkb):
                        P.op("pe", lambda e, kb=kb: e.matmul(psum[5], emat[:, kb * 128:(kb + 1) * 128], selT, start=True, stop=True),
                             reads=[selTb, mb], writes=[psb[5]])
                        if kb >= 4 * qb:
                            P.op("dve", lambda e, kb=kb: e.tensor_tensor(msk[:, kb, :], psum[5], Mc(kb - 4 * qb), ALU.mult),
                                 reads=[psb[5], mb], writes=[mskb[kb]])
                        else:
                            P.op("dve", lambda e, kb=kb: e.tensor_copy(msk[:, kb, :], psum[5]), reads=[psb[5]], writes=[mskb[kb]])
                    for r in range(4):
                        h = g * 4 + r
                        P.dma("sp", gbc, AP(gT_d, (h * 3) * T + qb * 512, [[0, 128], [T, 3], [1, 512]]), reads=[gT_b], writes=[gbcb])
                        P.op("dve", lambda e, r=r: e.tensor_tensor(acc, ocmp[r], gbc[:, 0, :], ALU.mult),
                             reads=[ocmpb[r], gbcb], writes=[accb])
                        pairs = [{"k": ksel[:, kb * 128:(kb + 1) * 128], "kb": [kselb], "v": vsel[:, kb, :], "vb": [vselb],
                                  "mask": msk[:, kb, :], "maskb": [mskb[kb]]} for kb in range(nkb)]
                        flash(pairs, hq4[r][:, qs], [hq4b[r]], 512, sc, 2, 4)
                        normalize(2, 4, 512, tq[0], tqb[0], 0)
                        P.op("pool", lambda e: e.tensor_tensor(tq[0], tq[0], gbc[:, 1, :], ALU.mult), reads=[tqb[0], gbcb], writes=[tqb[0]])
                        P.op("pool", lambda e: e.tensor_tensor(acc, acc, tq[0], ALU.add), reads=[tqb[0], accb], writes=[accb])
                        pairs = []
                        for kb in range(max(0, 4 * qb - 4), 4 * qb + 4):
                            m = kb - 4 * qb
                            pairs.append({"k": kwin[:, kb * 128:(kb + 1) * 128], "kb": [kwinb], "v": vwin[:, kb, :], "vb": [vwinb],
                                          "mask": (Mc(m) if m >= 0 else Mw(m + 4)), "maskb": [mb]})
                        flash(pairs, hq4[r][:, qs], [hq4b[r]], 512, sc, 3, 5)
                        normalize(3, 5, 512, tq[1], tqb[1], 1)
                        P.op("pool", lambda e: e.tensor_tensor(tq[1], tq[1], gbc[:, 2, :], ALU.mult), reads=[tqb[1], gbcb], writes=[tqb[1]])
                        o_ = cnt["ob"] % 2
                        cnt["ob"] += 1
                        P.op("dve", lambda e, o=ob[o_]: e.tensor_tensor(o, acc, tq[1], ALU.add), reads=[accb, tqb[1]], writes=[obb[o_]])
                        store_o(15 + h, qs, ob[o_], obb[o_])

        def dil_attn():
            sc = 128.0 ** -0.5
            num = ar.alloc([128, T], F32)
            numb = Buf("num")
            den = ar.alloc([128, T], F32)
            denb = Buf("den")
            qc_ = ar.alloc([128, T], BF16)
            qcb = Buf("qc")
            kc_ = ar.alloc([128, T], BF16)
            kcb = Buf("kc")
            vcl = ar.alloc([128, 16, 128], BF16)
            vclb = Buf("vcl")
            obig = ar.alloc([128, T], BF16)
            obigb = Buf("obig")
            pats = ((128, 1), (512, 4), (2048, 16))
            for j in (range(3) if heads is None else heads.get("d", [])):
                for g, (wn, r) in enumerate(pats):
                    hh = g * 3 + j
                    L = T // r
                    nblk = L // 128
                    s = load_head(ZQK["qd"] + hh, ZQK["kd"] + hh, None)
                    if r == 1:
                        qv, kv = hq[s], hk[s]
                        qvb, kvb = hqb[s], hkb[s]
                    else:
                        P.op("dve", lambda e, s=s, r=r: e.tensor_copy(qc_.rearrange("p (a i) -> p a i", a=r),
                                                                      hq[s].rearrange("p (i a) -> p a i", a=r)),
                             reads=[hqb[s]], writes=[qcb])
                        P.op("pool", lambda e, s=s, r=r: e.tensor_copy(kc_.rearrange("p (a i) -> p a i", a=r),
                                                                       hk[s].rearrange("p (i a) -> p a i", a=r)),
                             reads=[hkb[s]], writes=[kcb])
                        qv, kv, qvb, kvb = qc_, kc_, qcb, kcb
                    vcol = ZV["vd"] + hh * 128
                    for a in range(r):
                        src_ap = AP(zv_d, a * NV + vcol, [[r * NV, 128], [128 * r * NV, nblk], [1, 128]])
                        P.dma("sp", vcl[:, a * nblk:(a + 1) * nblk, :], src_ap, reads=[zv_b], writes=[vclb])
                    numv = num.rearrange("p (i a) -> p a i", a=r)
                    denv = den.rearrange("p (i a) -> p a i", a=r)
                    for a in range(r):
                        for bq in range(nblk):
                            pairs = []
                            for kb in ([bq - 1, bq] if bq > 0 else [bq]):
                                c0 = a * L + kb * 128
                                pairs.append({"k": kv[:, c0:c0 + 128], "kb": [kvb], "v": vcl[:, a * nblk + kb, :], "vb": [vclb],
                                              "mask": (Umask if kb < bq else Mc(0, 128)), "maskb": [mb]})
                            q0 = a * L + bq * 128
                            par = (a * nblk + bq) % 2
                            flash(pairs, qv[:, q0:q0 + 128], [qvb], 128, sc, 2 + par, 4 + par)
                            dn = numv[:, a, bq * 128:(bq + 1) * 128]
                            dd = denv[:, a, bq * 128:(bq + 1) * 128]
                            if g == 0:
                                P.op("dve", lambda e, o=dn, i_=psum[2 + par][:, 0:128]: e.tensor_copy(o, i_), reads=[psb[2 + par]], writes=[numb])
                                P.op("dve", lambda e, o=dd, i_=psum[4 + par][:, 0:128]: e.tensor_copy(o, i_), reads=[psb[4 + par]], writes=[denb])
                            else:
                                P.op("dve", lambda e, o=dn, i_=psum[2 + par][:, 0:128]: e.tensor_tensor(o, o, i_, ALU.add),
                                     reads=[psb[2 + par], numb], writes=[numb])
                                P.op("dve", lambda e, o=dd, i_=psum[4 + par][:, 0:128]: e.tensor_tensor(o, o, i_, ALU.add),
                                     reads=[psb[4 + par], denb], writes=[denb])
                for c in range(4):
                    cs = slice(c * 512, (c + 1) * 512)
                    P.op("dve", lambda e, cs=cs: e.reciprocal(den[:, cs], den[:, cs]), reads=[denb], writes=[denb])
                    P.op("pool", lambda e, cs=cs: e.tensor_tensor(obig[:, cs], num[:, cs], den[:, cs], ALU.mult),
                         reads=[numb, denb], writes=[obigb])
                store_o(23 + j, slice(0, T), obig, obigb)

        mark = ar.off
        diff_attn()
        P.barrier(); ar.off = mark
        fox_attn()
        P.barrier(); ar.off = mark
        nsa_attn()
        P.barrier(); ar.off = mark
        dil_attn()


    if mode == "ffn_only":
        pass
    load_x()
    for li in range(n_layers):
        if "ffn1" in phases:
            ffn(li, 1)
        if "mix" in phases:
            if mode != "attn":
                w_in_phase(li)
            if mode != "win":
                attention_phase(li)
                if mode != "attn":
                    w_out_phase(li)
        if "ffn2" in phases:
            ffn(li, 2)
        if "ple" in phases:
            ple(li)
    store_out()
    P.barrier()
    P.emit()
    return nc, P


WEIGHT_NAMES = ["ffn1_w_gate", "ffn1_w_up", "ffn1_w_down", "ffn2_w_gate", "ffn2_w_up", "ffn2_w_down",
                "w_in", "fox_bf", "diff_lam_q1", "diff_lam_k1", "diff_lam_q2", "diff_lam_k2", "diff_subln_g",
                "nsa_pe_k", "nsa_pe_v", "nsa_wk1", "nsa_wk2", "nsa_wv1", "nsa_wv2", "w_out",
                "ple_w_gate", "ple_w_proj"]


def prep_shared(inputs, names=WEIGHT_NAMES):
    g = np.stack([np.asarray(inputs[n], np.float32) for n in GAIN_NAMES], axis=1)
    g = g.reshape(NL, 8, KC, 128).transpose(3, 0, 1, 2).reshape(128, NL * 8 * KC)
    sh = {"gains": np.ascontiguousarray(g), "consts": make_consts(), "masks": make_masks(), "rope": make_rope()}
    for nm in names:
        sh[nm] = np.ascontiguousarray(inputs[nm], dtype=np.float32)
    return sh


def kernel(**inputs):
    nc, P = build()
    sh = prep_shared(inputs)
    x = np.ascontiguousarray(inputs["x"], dtype=np.float32)
    p = np.ascontiguousarray(inputs["p"], dtype=np.float32)
    in_maps = []
    for c in range(8):
        m = dict(sh)
        m["x"] = x[c]
        m["p"] = np.ascontiguousarray(p[:, c])
        in_maps.append(m)
    res = run_bass_kernel_spmd(nc, in_maps, core_ids=list(range(8)))
    return np.stack([np.asarray(r["out"]) for r in res.results], axis=0).astype(np.float32)
```
